# Optimizing a Trainium2 kernel written in Bass

```python
import jax, jax.numpy as jnp
from jax import lax
import numpy as np

D_MODEL = 4096
BATCH = 2
SEQ = 4096
DEPTH = 2

N_META = 16
MIX_W = D_MODEL
RET_W = MIX_W // 2
POOL_W = MIX_W - RET_W
RET_HEADS = 8
RET_HEAD_DIM = RET_W // RET_HEADS
CHUNK = 128
META_PAD = (-N_META) % CHUNK
POOL_WINDOWS = (2, 4, 8, 16)
N_POOL_GROUPS = len(POOL_WINDOWS)
POOL_GROUP = POOL_W // N_POOL_GROUPS
IN_COLS = 4 * RET_W + POOL_W
D_FF = ((8 * D_MODEL // 3 + 255) // 256) * 256
CONV_W = 3
ROPE_BASE = 10000.0
EPS = 1e-6

kernel_name = "hymba_retention_pool_convglu"


def rms_norm(x, g):
    xf = x.astype(jnp.float32)
    y = xf * lax.rsqrt(jnp.mean(xf * xf, axis=-1, keepdims=True) + EPS)
    return (y * g.astype(jnp.float32)).astype(x.dtype)


def rotary(x, pos):
    half = x.shape[-1] // 2
    inv = jnp.power(ROPE_BASE, -jnp.arange(half, dtype=jnp.float32) / half)
    ang = pos.astype(jnp.float32)[:, None] * inv[None, :]
    cos = jnp.cos(ang)[None, :, None, :]
    sin = jnp.sin(ang)[None, :, None, :]
    xf = x.astype(jnp.float32)
    x1, x2 = xf[..., :half], xf[..., half:]
    return jnp.concatenate([x1 * cos - x2 * sin, x1 * sin + x2 * cos], axis=-1).astype(x.dtype)


def retention(q, k, v):
    B, T, H, Dh = q.shape
    N = T // CHUNK
    f32 = jnp.float32
    log_gamma = jnp.log1p(-jnp.exp2(-5.0 - jnp.arange(H, dtype=f32)))
    idx = jnp.arange(CHUNK, dtype=f32)
    rel = idx[:, None] - idx[None, :]
    intra_decay = jnp.where(rel[None] >= 0,
                            jnp.exp(log_gamma[:, None, None] * jnp.maximum(rel, 0.0)[None]), 0.0)
    q_decay = jnp.exp(log_gamma[None, :] * (idx + 1.0)[:, None])
    k_decay = jnp.exp(log_gamma[None, :] * (CHUNK - 1.0 - idx)[:, None])
    chunk_decay = jnp.exp(log_gamma * CHUNK)

    qc = q.astype(f32).reshape(B, N, CHUNK, H, Dh)
    kc = k.astype(f32).reshape(B, N, CHUNK, H, Dh)
    vc = v.astype(f32).reshape(B, N, CHUNK, H, Dh)

    scores = jnp.einsum('bnchd,bnshd->bnhcs', qc, kc) * intra_decay[None, None]
    intra = jnp.einsum('bnhcs,bnshe->bnche', scores, vc)

    def step(state, xs):
        qn, kn, vn = xs
        cross = jnp.einsum('bchd,bhde->bche', qn * q_decay[None, :, :, None], state)
        state = state * chunk_decay[None, :, None, None] + jnp.einsum(
            'bchd,bche->bhde', kn * k_decay[None, :, :, None], vn)
        return state, cross

    state0 = jnp.zeros((B, H, Dh, Dh), f32)
    _, cross = lax.scan(step, state0, (jnp.moveaxis(qc, 1, 0), jnp.moveaxis(kc, 1, 0), jnp.moveaxis(vc, 1, 0)))
    cross = jnp.moveaxis(cross, 0, 1)
    return (intra + cross).reshape(B, T, H, Dh)


def multi_scale_pool(p, pool_w, pool_scale):
    B, L, _ = p.shape
    t = jnp.arange(L)
    outs = []
    for gi, w in enumerate(POOL_WINDOWS):
        xg = p[..., gi * POOL_GROUP:(gi + 1) * POOL_GROUP].astype(jnp.float32)
        cs = jnp.cumsum(xg, axis=1)
        cs_lag = jnp.pad(cs, ((0, 0), (w, 0), (0, 0)))[:, :L]
        cnt = jnp.minimum(t + 1, w).astype(jnp.float32)[None, :, None]
        mix = ((cs - cs_lag) / cnt - xg).astype(p.dtype)
        outs.append(jnp.einsum('blc,cd->bld', mix, pool_w[gi]))
    return jnp.concatenate(outs, axis=-1) * pool_scale


def hybrid_mixer(hn, w_in, pool_w, pool_scale, w_out, pos):
    B, L, _ = hn.shape
    proj = hn @ w_in
    q = proj[..., 0:RET_W].reshape(B, L, RET_HEADS, RET_HEAD_DIM)
    k = proj[..., RET_W:2 * RET_W].reshape(B, L, RET_HEADS, RET_HEAD_DIM)
    v = proj[..., 2 * RET_W:3 * RET_W].reshape(B, L, RET_HEADS, RET_HEAD_DIM)
    g = proj[..., 3 * RET_W:4 * RET_W]
    p = proj[..., 4 * RET_W:]

    q = rotary(q, pos)
    k = rotary(k, pos) * (RET_HEAD_DIM ** -0.5)
    pad = ((0, 0), (META_PAD, 0), (0, 0), (0, 0))
    r = retention(jnp.pad(q, pad), jnp.pad(k, pad), jnp.pad(v, pad))[:, META_PAD:]
    r = r * lax.rsqrt(jnp.mean(r * r, axis=-1, keepdims=True) + EPS)
    r = (r.reshape(B, L, RET_W) * jax.nn.silu(g.astype(jnp.float32))).astype(hn.dtype)

    m = multi_scale_pool(p, pool_w, pool_scale)

    return jnp.concatenate([r, m], axis=-1) @ w_out


def conv_glu_ffn(x, w_up, conv_w, conv_b, w_down):
    L = x.shape[1]
    u = x @ w_up
    a, b = u[..., :D_FF], u[..., D_FF:]
    ap = jnp.pad(a, ((0, 0), (CONV_W - 1, 0), (0, 0)))
    ac = conv_b
    for tap in range(CONV_W):
        ac = ac + ap[:, tap:tap + L] * conv_w[tap]
    return (jax.nn.silu(ac) * b) @ w_down


def setup_inputs(seed: int = 0) -> dict:
    key = jax.random.key(seed)
    ks = jax.random.split(key, 13)
    f32 = jnp.float32

    def nrm(k, shape, scale):
        return jax.random.normal(k, shape, f32) * scale

    return {
        "x": nrm(ks[0], (BATCH, SEQ, D_MODEL), 1.0),
        "meta_tokens": nrm(ks[1], (N_META, D_MODEL), 1.0),
        "norm1_g": 1.0 + nrm(ks[2], (DEPTH, D_MODEL), 0.02),
        "w_in": nrm(ks[3], (DEPTH, D_MODEL, IN_COLS), D_MODEL ** -0.5),
        "pool_w": nrm(ks[4], (DEPTH, N_POOL_GROUPS, POOL_GROUP, POOL_GROUP), POOL_GROUP ** -0.5),
        "pool_scale": 1.0 + nrm(ks[5], (DEPTH, POOL_W), 0.02),
        "w_out": nrm(ks[6], (DEPTH, MIX_W, D_MODEL), MIX_W ** -0.5),
        "norm2_g": 1.0 + nrm(ks[7], (DEPTH, D_MODEL), 0.02),
        "w_up": nrm(ks[8], (DEPTH, D_MODEL, 2 * D_FF), D_MODEL ** -0.5),
        "conv_w": nrm(ks[9], (DEPTH, CONV_W, D_FF), CONV_W ** -0.5),
        "conv_b": nrm(ks[10], (DEPTH, D_FF), 0.01),
        "w_down": nrm(ks[11], (DEPTH, D_FF, D_MODEL), D_FF ** -0.5),
        "final_g": 1.0 + nrm(ks[12], (D_MODEL,), 0.02),
    }


def reference(x, meta_tokens, norm1_g, w_in, pool_w, pool_scale, w_out, norm2_g, w_up, conv_w, conv_b, w_down, final_g):
    B = x.shape[0]
    meta = jnp.broadcast_to(meta_tokens[None].astype(x.dtype), (B, N_META, D_MODEL))
    h = jnp.concatenate([meta, x], axis=1)
    pos = jnp.arange(h.shape[1])
    for l in range(DEPTH):
        h = h + hybrid_mixer(rms_norm(h, norm1_g[l]), w_in[l], pool_w[l], pool_scale[l], w_out[l], pos)
        h = h + conv_glu_ffn(rms_norm(h, norm2_g[l]), w_up[l], conv_w[l], conv_b[l], w_down[l])
    return rms_norm(h, final_g)[:, N_META:]
```

```python
import math
import numpy as np
import concourse.bass as bass
import concourse.mybir as mybir
from concourse.bass_utils import run_bass_kernel_spmd

F32 = mybir.dt.float32
BF16 = mybir.dt.bfloat16
AF = mybir.ActivationFunctionType
ALU = mybir.AluOpType

ENGS = ("pe", "act", "dve", "pool", "sp")
NCORE = 8
QPB = 4
PRE = 16
DH = 256
EPS = 1e-6
SB_LO = 16512
SB_HI = 229344


class Cfg:
    def __init__(self, D=4096, H=8, PG=512, F=11008, NCH=8, L=2, KGMAX=43):
        self.D, self.H, self.PG, self.F, self.NCH, self.L = D, H, PG, F, NCH, L
        self.RW = H * DH
        self.PW = 4 * PG
        self.MIXW = self.RW + self.PW
        self.INC = 4 * self.RW + self.PW
        self.T = PRE + NCH * 128
        self.CH = NCH * 128
        self.KT = D // 128
        self.MT = self.MIXW // 128
        self.FT = F // 128
        self.RWT = self.RW // 128
        self.PWT = self.PW // 128
        self.PGT = PG // 128
        ng = -(-self.FT // KGMAX)
        per = -(-self.FT // ng)
        self.kgroups = [(g * per, min(per, self.FT - g * per)) for g in range(ng)]
        cols = {}
        o = 0
        for name, n in (("g1", L * self.KT), ("g2", L * self.KT), ("gf", self.KT), ("psc", L * self.PWT),
                        ("cw", L * 3 * self.FT), ("cb", L * self.FT), ("coef", NCORE * H), ("sel", NCORE),
                        ("invc", 4 * PRE), ("pm", 1), ("npm", 1), ("cmask", 128), ("ident", 128), ("ones", 128), ("g128", H)):
            cols[name] = (o, n)
            o += n
        self.tabA = cols
        self.nA = o
        colsB = {}
        o = 0
        for name, n in (("cos", self.T), ("sin", self.T), ("qdec", H * 128), ("kdec", H * 128)):
            colsB[name] = (o, n)
            o += n
        self.tabB = colsB
        self.nB = o


def groups_of(tt):
    ng = -(-tt // 512)
    base, rem = divmod(tt, ng)
    out, c = [], 0
    for i in range(ng):
        n = base + (1 if i < rem else 0)
        out.append((c, n))
        c += n
    return out


class Res:
    def __init__(self, name):
        self.name = name
        self.w = {}
        self.wfull = {}
        self.r = {}
        self.sem = None
        self.semn = 0


class Tile(Res):
    def __init__(self, name, t):
        super().__init__(name)
        self.t = t

    def __getitem__(self, idx):
        return self.t[idx]


class Prog:
    def __init__(self, nc):
        self.nc = nc
        self.q = {e: [] for e in ENGS}
        self.sems = []
        self.semval = []
        self.nobar = set()
        self.esem = {}
        self.waited = {e: {} for e in ENGS}
        for e in ENGS:
            self.esem[e] = self.new_sem("eng_" + e)
        self.uid = 0
        self.sb_persist = SB_LO
        self.sb_ptr = SB_LO
        self.sb_max = 0
        self.dma_sem_pool = []
        self.stage_sems = []
        self.nops = 0
        self.nwaits = 0

    def new_sem(self, name):
        h = self.nc.alloc_semaphore(name=f"{name}{len(self.sems)}")
        self.sems.append(h)
        self.semval.append(0)
        return len(self.sems) - 1

    def _alloc(self, name, shape, dtype):
        nbytes = int(np.prod(shape[1:])) * (4 if dtype == F32 else 2)
        off = (self.sb_ptr + 63) // 64 * 64
        assert off + nbytes <= SB_HI, f"SBUF overflow allocating {name}: {off}+{nbytes}"
        self.uid += 1
        t = self.nc.alloc_sbuf_tensor_at(f"{name}_{self.uid}", list(shape), dtype, offset=off)
        self.sb_ptr = off + nbytes
        self.sb_max = max(self.sb_max, self.sb_ptr)
        return Tile(name, t)

    def persist(self, name, shape, dtype):
        assert self.sb_ptr == self.sb_persist
        t = self._alloc(name, shape, dtype)
        self.sb_persist = self.sb_ptr
        return t

    def sb(self, name, shape, dtype):
        t = self._alloc(name, shape, dtype)
        return t

    def psum(self, name):
        self.uid += 1
        t = self.nc.alloc_psum_tensor(f"{name}_{self.uid}", [128, 512], F32)
        return Tile(name, t)

    def dram(self, name, shape, dtype):
        t = self.nc.dram_tensor(name, list(shape), dtype, kind="Internal")
        return Tile(name, t)

    def _collect(self, eng, reads, writes, acc_writes):
        need = {}

        def merge(d):
            for s, v in d.items():
                if need.get(s, 0) < v:
                    need[s] = v
        for r in reads:
            merge(r.w)
        for w in writes:
            merge(w.w)
            merge(w.r)
        for w in acc_writes:
            merge(w.r)
            merge(w.wfull)
        wd = self.waited[eng]
        own = self.esem[eng]
        waits = []
        for s, v in need.items():
            if s == own and eng == "pe":
                continue
            if wd.get(s, 0) < v:
                wd[s] = v
                waits.append((s, v))
        return waits

    def _commit(self, ticket, reads, writes, acc_writes):
        s, v = ticket
        for r in reads:
            if r.r.get(s, 0) < v:
                r.r[s] = v
        for w in writes:
            w.w = {s: v}
            w.wfull = {s: v}
            w.r = {}
        for w in acc_writes:
            if w.w.get(s, 0) < v:
                w.w[s] = v

    def op(self, eng, fn, reads=(), writes=(), acc=()):
        waits = self._collect(eng, reads, writes, acc)
        s = self.esem[eng]
        self.semval[s] += 1
        ticket = (s, self.semval[s])
        sems = self.sems
        self.nops += 1
        self.nwaits += len(waits)

        def emit(e):
            for ws, wv in waits:
                e.wait_ge(sems[ws], wv)
            fn(e).then_inc(sems[s], 1)
        self.q[eng].append(emit)
        self._commit(ticket, reads, writes, acc)

    def dma(self, eng, out, in_, semres, reads=(), writes=(), acc=()):
        waits = self._collect(eng, reads, writes, acc)
        if semres.sem is None:
            semres.sem = self.dma_sem_pool.pop() if self.dma_sem_pool else self.new_sem("d")
            self.stage_sems.append(semres.sem)
        s = semres.sem
        self.semval[s] += 16
        ticket = (s, self.semval[s])
        sems = self.sems
        self.nops += 1
        self.nwaits += len(waits)

        def emit(e):
            for ws, wv in waits:
                e.wait_ge(sems[ws], wv)
            e.dma_start(out=out, in_=in_).then_inc(sems[s], 16)
        self.q[eng].append(emit)
        self._commit(ticket, reads, writes, acc)

    def collective(self, src, dst, nobar=False):
        waits = self._collect("pool", [src], [dst], [])
        s = self.new_sem("cc")
        if nobar:
            self.nobar.add(s)
        self.semval[s] = 1
        sems = self.sems

        def emit(e):
            for ws, wv in waits:
                e.wait_ge(sems[ws], wv)
            e.collective_compute("AllGather", ALU.bypass, replica_groups=[list(range(NCORE))],
                                 ins=[src.t.ap()], outs=[dst.t.ap()],
                                 dma_qos=("P2" if nobar else None)).then_inc(sems[s], 1)
        self.q["pool"].append(emit)
        self._commit((s, 1), [src], [dst], [])

    def barrier(self, engines=ENGS):
        sems = self.sems
        for eng in engines:
            wd = self.waited[eng]
            waits = []
            for s, v in enumerate(self.semval):
                if s in self.nobar or v == 0:
                    continue
                if s == self.esem[eng]:
                    continue
                if wd.get(s, 0) < v:
                    wd[s] = v
                    waits.append((s, v))
            self.nwaits += len(waits)

            def emit(e, waits=waits):
                for ws, wv in waits:
                    e.wait_ge(sems[ws], wv)
            self.q[eng].append(emit)

    def stage(self):
        self.barrier()
        self.sb_ptr = self.sb_persist
        self.dma_sem_pool.extend(self.stage_sems)
        self.stage_sems = []

    def build(self):
        q = self.q
        with self.nc.Block() as block:
            @block.tensor
            def _(e):
                for f in q["pe"]:
                    f(e)

            @block.scalar
            def _(e):
                for f in q["act"]:
                    f(e)

            @block.vector
            def _(e):
                for f in q["dve"]:
                    f(e)

            @block.gpsimd
            def _(e):
                for f in q["pool"]:
                    f(e)

            @block.sync
            def _(e):
                for f in q["sp"]:
                    f(e)


class Builder:
    def __init__(self, cfg, dump=()):
        self.c = cfg
        self.dump = set(dump)
        c = cfg
        nc = bass.Bass("TRN2", target_bir_lowering=False)
        self.nc = nc
        P = Prog(nc)
        self.P = P
        L = c.L

        def ext(name, shape, dt=F32):
            return nc.dram_tensor(name, list(shape), dt, kind="ExternalInput")
        self.xT = Tile("xT", ext("xT", [c.D, c.T]))
        self.tabA_d = ext("tabA", [128, c.nA])
        self.tabB_d = ext("tabB", [128, c.nB])
        self.wsh = {}
        self.wfull = {}
        wshapes = {"w_in": (c.D, c.INC), "pool_w": (4 * c.PG, c.PG), "w_out": (c.MIXW, c.D),
                   "w_up": (c.D, 2 * c.F), "w_down": (c.F, c.D)}
        self.wshapes = wshapes
        for l in range(L):
            for k, (r, n) in wshapes.items():
                self.wsh[(k, l)] = Tile(f"{k}{l}_s", ext(f"{k}{l}_s", [r // NCORE, n]))
        self.outT = nc.dram_tensor("outT", [c.D, c.CH], F32, kind="ExternalOutput")
        self.out_res = Res("out")
        self.dump_out = {}

        self.tabA = P.persist("tabA", [128, c.nA], F32)
        self.identb = P.persist("identb", [128, 128], BF16)
        self.onesb = P.persist("onesb", [128, 128], BF16)
        self.ph = P.persist("ph", [128, c.PWT, PRE], F32)
        self.hh = P.persist("hh", [128, c.KT, 2], F32)
        self.ps = [P.psum(f"ps{i}") for i in range(8)]

        self.hmid = P.dram("hmid", [c.D, c.T], F32)
        self.hacc = P.dram("hacc", [c.D, c.T], F32)
        self.hout = [P.dram(f"hout{l}", [c.D, c.T], F32) for l in range(L)]
        self.qT = P.dram("qT", [c.RW, c.T], BF16)
        self.kT = P.dram("kT", [c.RW, c.T], BF16)
        self.vT = P.dram("vT", [c.RW, c.T], BF16)
        self.sg = P.dram("sg", [c.RW, c.T], F32)
        self.pT = P.dram("pT", [c.PW, c.T], F32)
        self.mixin = P.dram("mixin", [c.MIXW, c.T], BF16)
        self.gated = P.dram("gated", [c.F, c.T], BF16)
        self.exS = [P.dram(f"exS{l}", [c.H * 256, 256], F32) for l in range(L)]
        self.gS = [P.dram(f"gS{l}", [NCORE * c.H * 256, 256], F32) for l in range(L)]
        self.exP = [P.dram(f"exP{l}", [128, c.PWT * PRE], F32) for l in range(L)]
        self.gP = [P.dram(f"gP{l}", [NCORE * 128, c.PWT * PRE], F32) for l in range(L)]
        self.exH = [P.dram(f"exH{l}", [128, c.KT * 2], F32) for l in range(L)]
        self.gH = [P.dram(f"gH{l}", [NCORE * 128, c.KT * 2], F32) for l in range(L)]
        for l in range(L):
            for k, (r, n) in wshapes.items():
                self.wfull[(k, l)] = P.dram(f"{k}{l}_g", [r, n], BF16)
        self.wprep_done = set()

    def A(self, name, i=0, n=1):
        o, _ = self.c.tabA[name]
        return self.tabA[:, o + i:o + i + n]

    def dump_tile(self, name, dram_tile):
        if name not in self.dump:
            return
        P = self.P
        shape = list(dram_tile.t.shape)
        o = self.nc.dram_tensor("dbg_" + name, shape, dram_tile.t.dtype, kind="ExternalOutput")
        r = Tile("dbg_" + name, o)
        P.dma("sp", o.ap(), dram_tile.t.ap(), r, reads=[dram_tile], writes=[r])
        self.dump_out[name] = r

    def wprep(self, key):
        if key in self.wprep_done:
            return
        self.wprep_done.add(key)
        P = self.P
        sh = self.wsh[key]
        r, n = sh.t.shape
        tmp = P.dram(f"{sh.name}_b", [r, n], BF16)
        tmp.sem = P.new_sem("wc")
        P.nobar.add(tmp.sem)
        P.dma("pool", tmp.t.ap(), sh.t.ap(), tmp, writes=[tmp])
        P.collective(tmp, self.wfull[key], nobar=True)

    def wrows(self, key, k0, nk):
        full = self.wfull[key]
        v = full.t.ap().rearrange("(kt p) n -> p kt n", p=128)
        return [(full, v[:, k0:k0 + nk, :], 0, nk)]

    def gemm(self, key, xb, k0, nk, col_list, groups, epi, nslots=3, NW=256):
        P = self.P
        ng = len(groups)
        slots = [P.sb(f"wslot{i}", [128, nk, NW], BF16) for i in range(nslots)]
        slabs = []
        i = 0
        while i < len(col_list):
            j = i + 1
            while j < len(col_list) and col_list[j] == col_list[j - 1] + 128 and (j - i) < NW // 128:
                j += 1
            slabs.append((i, j))
            i = j
        srcs = self.wrows(key, k0, nk)

        def load(si):
            a, b = slabs[si]
            slot = slots[si % nslots]
            c0 = col_list[a]
            w = (b - a) * 128
            for (res, v, ko, kc) in srcs:
                for k1 in range(0, kc, 16):
                    k2 = min(kc, k1 + 16)
                    P.dma("sp", slot[:, ko + k1:ko + k2, 0:w], v[:, k1:k2, c0:c0 + w], slot, reads=[res],
                          writes=[slot] if (ko + k1) == 0 else (), acc=[slot] if (ko + k1) else ())
        for si in range(min(nslots - 1, len(slabs))):
            load(si)
        it = 0
        for si, (a, b) in enumerate(slabs):
            if si + nslots - 1 < len(slabs):
                load(si + nslots - 1)
            slot = slots[si % nslots]
            for i in range(a, b):
                banks = self.ps[(it % 2) * ng:(it % 2) * ng + ng]
                j = i - a

                def mm(e, slot=slot, j=j, banks=banks):
                    ins = None
                    for kt in range(nk):
                        for gi, (g0, gn) in enumerate(groups):
                            ins = e.matmul(banks[gi][:, 0:gn], lhsT=slot[:, kt, j * 128:(j + 1) * 128],
                                           rhs=xb[:, kt, g0:g0 + gn], start=(kt == 0), stop=(kt == nk - 1))
                    return ins
                P.op("pe", mm, reads=[slot, xb], writes=banks)
                epi(i, banks)
                it += 1

    def evac(self, banks, groups, dst, dst_off=0, first_write=True):
        P = self.P
        for gi, (g0, gn) in enumerate(groups):
            b = banks[gi]
            w = [dst] if (gi == 0 and first_write) else ()
            a = () if (gi == 0 and first_write) else [dst]
            if gi % 2 == 0:
                P.op("act", lambda e, b=b, g0=g0, gn=gn: e.copy(out=dst[:, dst_off + g0:dst_off + g0 + gn], in_=b[:, 0:gn]),
                     reads=[b], writes=w, acc=a)
            else:
                P.op("dve", lambda e, b=b, g0=g0, gn=gn: e.tensor_copy(out=dst[:, dst_off + g0:dst_off + g0 + gn], in_=b[:, 0:gn]),
                     reads=[b], writes=w, acc=a)

    def rmsnorm(self, src, gain, tt, halo, out_fn):
        P, c = self.P, self.c
        assert tt == c.T
        groups = groups_of(tt)
        hx = [P.sb(f"hx{i}", [128, tt], F32) for i in range(3)]
        sq = [P.sb(f"sq{i}", [128, tt], BF16) for i in range(2)]
        rstd = P.sb("rstd", [128, tt], F32)
        ssb = self.ps[0:len(groups)]

        def load(kt, t):
            P.dma("sp", t[:, :], src.t.ap()[kt * 128:(kt + 1) * 128, :], t, reads=[src], writes=[t])
            if halo is not None:
                P.op("dve", lambda e: e.tensor_tensor(out=t[:, PRE - 2:PRE], in0=t[:, PRE - 2:PRE], in1=halo[:, kt, :], op=ALU.add),
                     reads=[halo], writes=[t])
        for kt in range(c.KT):
            t = hx[kt % 3]
            s = sq[kt % 2]
            load(kt, t)
            P.op("act", lambda e, t=t, s=s: e.activation(out=s[:, :], in_=t[:, :], func=AF.Square), reads=[t], writes=[s])

            def mm(e, s=s, kt=kt):
                ins = None
                for gi, (g0, gn) in enumerate(groups):
                    ins = e.matmul(ssb[gi][:, 0:gn], lhsT=self.onesb[:, :], rhs=s[:, g0:g0 + gn],
                                   start=(kt == 0), stop=(kt == c.KT - 1))
                return ins
            P.op("pe", mm, reads=[s, self.onesb], writes=ssb if kt == 0 else (), acc=ssb if kt else ())
        for gi, (g0, gn) in enumerate(groups):
            P.op("act", lambda e, gi=gi, g0=g0, gn=gn: e.activation(out=rstd[:, g0:g0 + gn], in_=ssb[gi][:, 0:gn], func=AF.Sqrt,
                                                                  scale=1.0 / c.D, bias=EPS),
                 reads=[ssb[gi]], writes=[rstd] if gi == 0 else (), acc=[rstd] if gi else ())
        P.op("dve", lambda e: e.reciprocal(out=rstd[:, :], in_=rstd[:, :]), writes=[rstd])
        for kt in range(c.KT):
            t = hx[kt % 3]
            load(kt, t)
            out_fn(kt, t, rstd)

    def norm_to_xb(self, src, gname, l, tt, halo):
        P, c = self.P, self.c
        xb = P.sb("xb", [128, c.KT, tt], BF16)

        def out_fn(kt, t, rstd):
            g = self.A(gname, l * c.KT + kt)
            P.op("dve", lambda e: e.scalar_tensor_tensor(out=xb[:, kt, :], in0=t[:, :], scalar=g, in1=rstd[:, :],
                                                        op0=ALU.mult, op1=ALU.mult),
                 reads=[t, rstd], acc=[xb])
        self.rmsnorm(src, None, tt, halo, out_fn)
        return xb

    def stage_init(self):
        P, c = self.P, self.c
        P.dma("sp", self.tabA[:, :], self.tabA_d.ap(), self.tabA, writes=[self.tabA])
        o, _ = c.tabA["ident"]
        P.op("dve", lambda e: e.tensor_copy(out=self.identb[:, :], in_=self.tabA[:, o:o + 128]), reads=[self.tabA], writes=[self.identb])
        o2, _ = c.tabA["ones"]
        P.op("dve", lambda e: e.tensor_copy(out=self.onesb[:, :], in_=self.tabA[:, o2:o2 + 128]), reads=[self.tabA], writes=[self.onesb])

    def stage_inproj(self, l, hsrc):
        P, c = self.P, self.c
        P.stage()
        T = c.T
        groups = groups_of(T)
        tabB = P.sb("tabB", [128, c.nB], F32)
        P.dma("sp", tabB[:, :], self.tabB_d.ap(), tabB, writes=[tabB])

        def B(name, i=0, n=1):
            o, _ = c.tabB[name]
            return tabB[:, o + i:o + i + n]
        cos, sin = B("cos", 0, T), B("sin", 0, T)
        xb = self.norm_to_xb(hsrc, "g1", l, T, None)
        ob = [P.sb(f"ob{i}", [128, T], F32) for i in range(4)]
        t1 = P.sb("t1", [128, T], F32)
        t2 = P.sb("t2", [128, T], F32)
        o12 = [P.sb(f"o12_{i}", [128, T], F32) for i in range(2)]
        qb = [P.sb(f"qb{i}", [128, T], BF16) for i in range(4)]
        pe = P.sb("pe", [128, c.PWT, PRE], F32)
        state = {"obi": 0, "x1": None, "qbi": 0}
        RWT = c.RWT

        def rot(h, kind, x1, x2):
            dst = self.qT if kind == 0 else self.kT
            dname = "qdec" if kind == 0 else "kdec"
            dec = B(dname, h * 128, 128)
            for half in range(2):
                a, b_, op = (x1, x2, ALU.subtract) if half == 0 else (x1, x2, ALU.add)
                ta, tb = (cos, sin) if half == 0 else (sin, cos)
                P.op("dve", lambda e, a=a, ta=ta: e.tensor_tensor(out=t1[:, :], in0=a[:, :], in1=ta, op=ALU.mult), reads=[a, tabB], writes=[t1])
                P.op("dve", lambda e, b_=b_, tb=tb: e.tensor_tensor(out=t2[:, :], in0=b_[:, :], in1=tb, op=ALU.mult), reads=[b_, tabB], writes=[t2])
                o = o12[half]
                P.op("dve", lambda e, o=o, op=op: e.tensor_tensor(out=o[:, :], in0=t1[:, :], in1=t2[:, :], op=op), reads=[t1, t2], writes=[o])
                q = qb[state["qbi"] % 4]
                state["qbi"] += 1
                P.op("dve", lambda e, o=o, q=q: e.tensor_tensor(out=q[:, 0:PRE], in0=o[:, 0:PRE], in1=dec[:, 128 - PRE:128], op=ALU.mult),
                     reads=[o, tabB], writes=[q])
                P.op("dve", lambda e, o=o, q=q: e.tensor_tensor(
                    out=q[:, PRE:T].rearrange("p (c i) -> p c i", i=128), in0=o[:, PRE:T].rearrange("p (c i) -> p c i", i=128),
                    in1=dec.unsqueeze(1).broadcast_to([128, c.NCH, 128]), op=ALU.mult), reads=[o, tabB], acc=[q])
                r0 = h * 256 + half * 128
                P.dma("sp", dst.t.ap()[r0:r0 + 128, :], q[:, :], q, reads=[q], acc=[dst])

        def epi(nt, banks):
            kind = min(nt // RWT, 4)
            if kind in (0, 1):
                o = ob[state["obi"] % 4]
                state["obi"] += 1
                self.evac(banks, groups, o)
                if nt % 2 == 0:
                    state["x1"] = o
                else:
                    rot((nt % RWT) // 2, kind, state["x1"], o)
            elif kind == 2:
                q = qb[state["qbi"] % 4]
                state["qbi"] += 1
                self.evac(banks, groups, q)
                r0 = (nt - 2 * RWT) * 128
                P.dma("sp", self.vT.t.ap()[r0:r0 + 128, :], q[:, :], q, reads=[q], acc=[self.vT])
            elif kind == 3:
                o = ob[state["obi"] % 4]
                state["obi"] += 1
                for gi, (g0, gn) in enumerate(groups):
                    P.op("act", lambda e, gi=gi, g0=g0, gn=gn, o=o: e.activation(out=o[:, g0:g0 + gn], in_=banks[gi][:, 0:gn], func=AF.Silu),
                         reads=[banks[gi]], writes=[o] if gi == 0 else (), acc=[o] if gi else ())
                r0 = (nt - 3 * RWT) * 128
                P.dma("sp", self.sg.t.ap()[r0:r0 + 128, :], o[:, :], o, reads=[o], acc=[self.sg])
            else:
                o = ob[state["obi"] % 4]
                state["obi"] += 1
                self.evac(banks, groups, o)
                ft = nt - 4 * RWT
                P.op("act", lambda e, o=o, ft=ft: e.copy(out=pe[:, ft, :], in_=o[:, T - PRE:T]), reads=[o], acc=[pe])
                P.dma("sp", self.pT.t.ap()[ft * 128:(ft + 1) * 128, :], o[:, :], o, reads=[o], acc=[self.pT])
        self.gemm(("w_in", l), xb, 0, c.KT, [i * 128 for i in range(c.INC // 128)], groups, epi)
        P.dma("sp", self.exP[l].t.ap(), pe[:, :, :].rearrange("p a b -> p (a b)"), pe, reads=[pe], writes=[self.exP[l]])

    def chunks(self):
        c = self.c
        return [(0, PRE)] + [(PRE + i * 128, 128) for i in range(c.NCH)]

    def stage_retention(self, l, with_out):
        P, c = self.P, self.c
        P.stage()
        T = c.T
        groups = groups_of(T)
        chunks = self.chunks()
        NCK = len(chunks)
        cmask = self.A("cmask", 0, 128)
        psK, psV, psS, psR, psP = self.ps[0], self.ps[1], self.ps[2], self.ps[3], self.ps[4]
        ssb = self.ps[5:5 + len(groups)]
        assert len(groups) <= 3
        qh = [P.sb(f"qh{i}", [128, 2, T], BF16) for i in range(2)]
        kh = [P.sb(f"kh{i}", [128, 2, T], BF16) for i in range(2)]
        vh = [P.sb(f"vh{i}", [128, 2, T], BF16) for i in range(2)]
        ktm = P.sb("ktm", [128, NCK, 256], BF16)
        vtm = P.sb("vtm", [128, NCK, 256], BF16)
        Sf = P.sb("Sf", [128, 512], F32)
        Sb = P.sb("Sb", [128, 512], BF16)
        if with_out:
            mb = [P.sb(f"mb{i}", [128, 128], BF16) for i in range(2)]
            rT = P.sb("rT", [128, 2, T], F32)
            sqr = P.sb("sqr", [128, 2, T], BF16)
            rstd = P.sb("rstd", [128, T], F32)
            sgh = P.sb("sgh", [128, 2, T], F32)
            tmp = P.sb("tmpr", [128, 2, T], F32)
            rg = [P.sb(f"rg{i}", [128, 2, T], BF16) for i in range(2)]
            G = [P.sb(f"G{i}", [128, NCORE, 512], F32) for i in range(2)]
            gv = self.gS[l].t.ap().rearrange("(i h dt p) e -> p i h dt e", i=NCORE, h=c.H, dt=2, p=128)

        def hv(dt_, h):
            return dt_.t.ap()[h * 256:(h + 1) * 256, :].rearrange("(dt p) t -> p dt t", p=128)
        for h in range(c.H):
            q_, k_, v_ = qh[h % 2], kh[h % 2], vh[h % 2]
            P.dma("sp", k_[:, :, :], hv(self.kT, h), k_, reads=[self.kT], writes=[k_])
            P.dma("sp", v_[:, :, :], hv(self.vT, h), v_, reads=[self.vT], writes=[v_])
            if with_out:
                P.dma("sp", q_[:, :, :], hv(self.qT, h), q_, reads=[self.qT], writes=[q_])
            g128 = self.A("g128", h)
            for ci, (c0, n) in enumerate(chunks):
                for (src, psb, dstt, eng) in ((k_, psK, ktm, "act"), (v_, psV, vtm, "dve")):
                    def mm(e, src=src, psb=psb, c0=c0, n=n):
                        ins = None
                        for dt in range(2):
                            ins = e.matmul(psb[0:n, dt * 128:(dt + 1) * 128], lhsT=src[:, dt, c0:c0 + n], rhs=self.identb[:, :],
                                           start=True, stop=True)
                        return ins
                    P.op("pe", mm, reads=[src, self.identb], writes=[psb])
                    if eng == "act":
                        P.op("act", lambda e, psb=psb, dstt=dstt, ci=ci, n=n: e.copy(out=dstt[0:n, ci, :], in_=psb[0:n, 0:256]),
                             reads=[psb], writes=[dstt] if ci == 0 else (), acc=[dstt] if ci else ())
                    else:
                        P.op("dve", lambda e, psb=psb, dstt=dstt, ci=ci, n=n: e.tensor_copy(out=dstt[0:n, ci, :], in_=psb[0:n, 0:256]),
                             reads=[psb], writes=[dstt] if ci == 0 else (), acc=[dstt] if ci else ())
            if with_out:
                Gh = G[h % 2]
                for dt in range(2):
                    P.dma("sp", Gh[:, :, dt * 256:(dt + 1) * 256], gv[:, :, h, dt, :], Gh, reads=[self.gS[l]],
                          writes=[Gh] if dt == 0 else (), acc=[Gh] if dt else ())
                for i in range(NCORE):
                    cf = self.A("coef", i * c.H + h)
                    if i == 0:
                        P.op("dve", lambda e, Gh=Gh, cf=cf: e.tensor_scalar(out=Sf[:, :], in0=Gh[:, 0, :], scalar1=cf, scalar2=None, op0=ALU.mult),
                             reads=[Gh, self.tabA], writes=[Sf])
                    else:
                        P.op("dve", lambda e, Gh=Gh, cf=cf, i=i: e.scalar_tensor_tensor(out=Sf[:, :], in0=Gh[:, i, :], scalar=cf, in1=Sf[:, :],
                                                                                     op0=ALU.mult, op1=ALU.add),
                             reads=[Gh, self.tabA], writes=[Sf])
            else:
                P.op("dve", lambda e: e.memset(Sf[:, :], 0.0), writes=[Sf])
            P.op("act", lambda e: e.copy(out=Sb[:, :], in_=Sf[:, :]), reads=[Sf], writes=[Sb])
            for ci, (c0, n) in enumerate(chunks):
                if with_out:
                    def mms(e, c0=c0, n=n, k_=k_, q_=q_):
                        ins = None
                        for dt in range(2):
                            ins = e.matmul(psS[0:n, 0:n], lhsT=k_[:, dt, c0:c0 + n], rhs=q_[:, dt, c0:c0 + n], start=(dt == 0), stop=(dt == 1))
                        return ins
                    P.op("pe", mms, reads=[k_, q_], writes=[psS])
                    m = mb[ci % 2]
                    P.op("dve", lambda e, m=m, n=n: e.tensor_tensor(out=m[0:n, 0:n], in0=psS[0:n, 0:n], in1=cmask[0:n, 0:n], op=ALU.mult),
                         reads=[psS, self.tabA], writes=[m])

                    def mmr(e, m=m, c0=c0, n=n, ci=ci, q_=q_):
                        ins = None
                        for et in range(2):
                            ins = e.matmul(psR[:, et * 128:et * 128 + n], lhsT=vtm[0:n, ci, et * 128:(et + 1) * 128], rhs=m[0:n, 0:n],
                                           start=True, stop=False)
                            for dt in range(2):
                                ins = e.matmul(psR[:, et * 128:et * 128 + n], lhsT=Sb[:, dt * 256 + et * 128:dt * 256 + (et + 1) * 128],
                                               rhs=q_[:, dt, c0:c0 + n], start=False, stop=(dt == 1))
                        return ins
                    P.op("pe", mmr, reads=[vtm, m, Sb, q_], writes=[psR])
                    P.op("act", lambda e, c0=c0, n=n: e.copy(out=rT[:, :, c0:c0 + n], in_=psR[:, 0:256].rearrange("p (et t) -> p et t", et=2)[:, :, 0:n]),
                         reads=[psR], writes=[rT] if ci == 0 else (), acc=[rT] if ci else ())

                def mmp(e, ci=ci, n=n):
                    ins = None
                    for dt in range(2):
                        ins = e.matmul(psP[:, dt * 256:(dt + 1) * 256], lhsT=ktm[0:n, ci, dt * 128:(dt + 1) * 128], rhs=vtm[0:n, ci, :],
                                       start=True, stop=True)
                    return ins
                P.op("pe", mmp, reads=[ktm, vtm], writes=[psP])
                P.op("dve", lambda e, g128=g128: e.tensor_scalar(out=Sf[:, :], in0=Sf[:, :], scalar1=g128, scalar2=None, op0=ALU.mult),
                     reads=[self.tabA], writes=[Sf])
                P.op("dve", lambda e, g128=g128: e.scalar_tensor_tensor(out=Sf[:, :], in0=psP[:, :], scalar=g128, in1=Sf[:, :], op0=ALU.mult, op1=ALU.add),
                     reads=[psP, self.tabA], writes=[Sf])
                P.op("act", lambda e: e.copy(out=Sb[:, :], in_=Sf[:, :]), reads=[Sf], writes=[Sb])
            if not with_out:
                P.dma("sp", self.exS[l].t.ap()[h * 256:(h + 1) * 256, :].rearrange("(dt p) e -> p dt e", p=128),
                      Sf[:, :].rearrange("p (dt e) -> p dt e", dt=2), Sf, reads=[Sf], acc=[self.exS[l]])
            else:
                P.dma("sp", sgh[:, :, :], hv(self.sg, h), sgh, reads=[self.sg], writes=[sgh])
                P.op("act", lambda e: e.activation(out=sqr[:, :, :], in_=rT[:, :, :], func=AF.Square), reads=[rT], writes=[sqr])

                def mmn(e):
                    ins = None
                    for gi, (g0, gn) in enumerate(groups):
                        for et in range(2):
                            ins = e.matmul(ssb[gi][:, 0:gn], lhsT=self.onesb[:, :], rhs=sqr[:, et, g0:g0 + gn], start=(et == 0), stop=(et == 1))
                    return ins
                P.op("pe", mmn, reads=[sqr, self.onesb], writes=ssb)
                for gi, (g0, gn) in enumerate(groups):
                    P.op("act", lambda e, gi=gi, g0=g0, gn=gn: e.activation(out=rstd[:, g0:g0 + gn], in_=ssb[gi][:, 0:gn], func=AF.Sqrt,
                                                                          scale=1.0 / DH, bias=EPS),
                         reads=[ssb[gi]], writes=[rstd] if gi == 0 else (), acc=[rstd] if gi else ())
                P.op("dve", lambda e: e.reciprocal(out=rstd[:, :], in_=rstd[:, :]), writes=[rstd])
                r_ = rg[h % 2]
                for et in range(2):
                    P.op("dve", lambda e, et=et: e.tensor_tensor(out=tmp[:, et, :], in0=rT[:, et, :], in1=rstd[:, :], op=ALU.mult),
                         reads=[rT, rstd], writes=[tmp] if et == 0 else (), acc=[tmp] if et else ())
                    P.op("dve", lambda e, et=et, r_=r_: e.tensor_tensor(out=r_[:, et, :], in0=tmp[:, et, :], in1=sgh[:, et, :], op=ALU.mult),
                         reads=[tmp, sgh], writes=[r_] if et == 0 else (), acc=[r_] if et else ())
                P.dma("sp", hv(self.mixin, h), r_[:, :, :], r_, reads=[r_], acc=[self.mixin])

    def select_halo(self, gsrc, ncol, dst_ap_fn, dst_tile, name):
        P = self.P
        g = P.sb("g" + name, [128, NCORE, ncol], F32)
        P.dma("sp", g[:, :, :], gsrc.t.ap().rearrange("(i p) n -> p i n", p=128), g, reads=[gsrc], writes=[g])
        for i in range(NCORE):
            s = self.A("sel", i)
            if i == 0:
                P.op("dve", lambda e, s=s: e.tensor_scalar(out=dst_ap_fn(), in0=g[:, 0, :], scalar1=s, scalar2=None, op0=ALU.mult),
                     reads=[g, self.tabA], writes=[dst_tile])
            else:
                P.op("dve", lambda e, s=s, i=i: e.scalar_tensor_tensor(out=dst_ap_fn(), in0=g[:, i, :], scalar=s, in1=dst_ap_fn(),
                                                                     op0=ALU.mult, op1=ALU.add),
                     reads=[g, self.tabA], writes=[dst_tile])

    def stage_pool(self, l):
        P, c = self.P, self.c
        P.stage()
        T = c.T
        TE = T + PRE
        groups = groups_of(T)
        ng = len(groups)
        self.select_halo(self.gP[l], c.PWT * PRE, lambda: self.ph[:, :, :].rearrange("p a b -> p (a b)"), self.ph, "P")
        pw = P.sb("pw", [128, 4 * c.PGT, c.PG], BF16)
        wsrc = self.wfull[("pool_w", l)]
        P.dma("sp", pw[:, :, :], wsrc.t.ap().rearrange("(kt p) n -> p kt n", p=128), pw, reads=[wsrc], writes=[pw])
        A_ = [P.sb(f"pa{i}", [128, TE], F32) for i in range(2)]
        A2 = [P.sb(f"pa2{i}", [128, TE], F32) for i in range(2)]
        B_ = [P.sb(f"pb{i}", [128, TE], F32) for i in range(2)]
        C_ = [P.sb(f"pc{i}", [128, TE], F32) for i in range(2)]
        for t_ in A2:
            P.op("dve", lambda e, t_=t_: e.memset(t_[:, 0:PRE], 0.0), writes=[t_])
        for t_ in B_ + C_:
            P.op("dve", lambda e, t_=t_: e.memset(t_[:, :], 0.0), writes=[t_])
        tmp = P.sb("ptmp", [128, PRE], F32)
        mixb = [P.sb(f"mixb{i}", [128, c.PGT, T], BF16) for i in range(2)]
        mo = [P.sb(f"mo{i}", [128, T], BF16) for i in range(2)]
        npm = self.A("npm")
        it = 0
        for gi in range(4):
            w = 2 ** (gi + 1)
            mx = mixb[gi % 2]
            for ci in range(c.PGT):
                ft = gi * c.PGT + ci
                a = A_[ft % 2]
                a2 = A2[ft % 2]
                P.dma("sp", a[:, PRE:TE], self.pT.t.ap()[ft * 128:(ft + 1) * 128, :], a, reads=[self.pT], writes=[a])
                P.op("act", lambda e, a=a, a2=a2: e.copy(out=a2[:, 2 * PRE:TE], in_=a[:, 2 * PRE:TE]), reads=[a], acc=[a2])
                P.op("dve", lambda e, a=a, a2=a2, ft=ft: e.tensor_tensor(out=a2[:, PRE:2 * PRE], in0=a[:, PRE:2 * PRE], in1=self.ph[:, ft, :], op=ALU.add),
                     reads=[a, self.ph], writes=[a2])
                cur = a2
                bufs = [B_[ft % 2], C_[ft % 2]]
                k = 0
                step = 1
                while step < w:
                    oth = bufs[k % 2]
                    P.op("dve", lambda e, cur=cur, oth=oth, step=step: e.tensor_tensor(out=oth[:, step:TE], in0=cur[:, step:TE],
                                                                                       in1=cur[:, 0:TE - step], op=ALU.add),
                         reads=[cur], writes=[oth])
                    cur = oth
                    k += 1
                    step *= 2
                s_ = cur
                P.op("dve", lambda e, s_=s_, a=a, ci=ci, mx=mx, w=w: e.scalar_tensor_tensor(out=mx[:, ci, PRE:T], in0=s_[:, 2 * PRE:TE], scalar=1.0 / w,
                                                                                        in1=a[:, 2 * PRE:TE], op0=ALU.mult, op1=ALU.subtract),
                     reads=[s_, a], writes=[mx] if ci == 0 else (), acc=[mx] if ci else ())
                iv = self.A("invc", gi * PRE, PRE)
                P.op("dve", lambda e, s_=s_, iv=iv: e.tensor_tensor(out=tmp[:, :], in0=s_[:, PRE:2 * PRE], in1=iv, op=ALU.mult),
                     reads=[s_, self.tabA], writes=[tmp])
                P.op("dve", lambda e, a=a, ci=ci, mx=mx: e.scalar_tensor_tensor(out=mx[:, ci, 0:PRE], in0=a[:, PRE:2 * PRE], scalar=npm, in1=tmp[:, :],
                                                                            op0=ALU.mult, op1=ALU.add),
                     reads=[tmp, a, self.tabA], acc=[mx])
            for co in range(c.PGT):
                banks = self.ps[(it % 2) * ng:(it % 2) * ng + ng]

                def mm(e, mx=mx, gi=gi, co=co, banks=banks):
                    ins = None
                    for ci in range(c.PGT):
                        for gj, (g0, gn) in enumerate(groups):
                            ins = e.matmul(banks[gj][:, 0:gn], lhsT=pw[:, gi * c.PGT + ci, co * 128:(co + 1) * 128], rhs=mx[:, ci, g0:g0 + gn],
                                           start=(ci == 0), stop=(ci == c.PGT - 1))
                    return ins
                P.op("pe", mm, reads=[pw, mx], writes=banks)
                o = mo[it % 2]
                ft = gi * c.PGT + co
                sc = self.A("psc", l * c.PWT + ft)
                for gj, (g0, gn) in enumerate(groups):
                    P.op("dve", lambda e, o=o, gj=gj, g0=g0, gn=gn, sc=sc, banks=banks: e.tensor_scalar(out=o[:, g0:g0 + gn], in0=banks[gj][:, 0:gn], scalar1=sc,
                                                                                                     scalar2=None, op0=ALU.mult),
                         reads=[banks[gj], self.tabA], writes=[o] if gj == 0 else (), acc=[o] if gj else ())
                r0 = c.RW + ft * 128
                P.dma("sp", self.mixin.t.ap()[r0:r0 + 128, :], o[:, :], o, reads=[o], acc=[self.mixin])
                it += 1

    def stage_wout(self, l, hsrc):
        P, c = self.P, self.c
        P.stage()
        T = c.T
        groups = groups_of(T)
        xb = P.sb("xb", [128, c.MT, T], BF16)
        mv = self.mixin.t.ap().rearrange("(kt p) t -> p kt t", p=128)
        for k1 in range(0, c.MT, 16):
            k2 = min(c.MT, k1 + 16)
            P.dma("sp", xb[:, k1:k2, :], mv[:, k1:k2, :], xb, reads=[self.mixin], writes=[xb] if k1 == 0 else (), acc=[xb] if k1 else ())
        hres = [P.sb(f"hres{i}", [128, T], F32) for i in range(2)]
        hm = [P.sb(f"hm{i}", [128, T], F32) for i in range(2)]
        e2 = P.sb("e2", [128, c.KT, 2], F32)

        def epi(nt, banks):
            hr = hres[nt % 2]
            o = hm[nt % 2]
            P.dma("sp", hr[:, :], hsrc.t.ap()[nt * 128:(nt + 1) * 128, :], hr, reads=[hsrc], writes=[hr])
            for gi, (g0, gn) in enumerate(groups):
                P.op("dve", lambda e, gi=gi, g0=g0, gn=gn: e.tensor_tensor(out=o[:, g0:g0 + gn], in0=banks[gi][:, 0:gn], in1=hr[:, g0:g0 + gn], op=ALU.add),
                     reads=[banks[gi], hr], writes=[o] if gi == 0 else (), acc=[o] if gi else ())
            P.op("act", lambda e: e.copy(out=e2[:, nt, :], in_=o[:, T - 2:T]), reads=[o], acc=[e2])
            P.dma("sp", self.hmid.t.ap()[nt * 128:(nt + 1) * 128, :], o[:, :], o, reads=[o], acc=[self.hmid])
        self.gemm(("w_out", l), xb, 0, c.MT, [i * 128 for i in range(c.KT)], groups, epi)
        P.dma("sp", self.exH[l].t.ap(), e2[:, :, :].rearrange("p a b -> p (a b)"), e2, reads=[e2], writes=[self.exH[l]])

    def stage_ffn_up(self, l):
        P, c = self.P, self.c
        P.stage()
        T = c.T
        groups = groups_of(T)
        self.select_halo(self.gH[l], c.KT * 2, lambda: self.hh[:, :, :].rearrange("p a b -> p (a b)"), self.hh, "H")
        xb = self.norm_to_xb(self.hmid, "g2", l, T, self.hh)
        asb = [P.sb(f"asb{i}", [128, T + 2], F32) for i in range(4)]
        for t_ in asb:
            P.op("dve", lambda e, t_=t_: e.memset(t_[:, 0:2], 0.0), writes=[t_])
        ac = [P.sb(f"ac{i}", [128, T], F32) for i in range(2)]
        sa = [P.sb(f"sa{i}", [128, T], F32) for i in range(2)]
        gt = [P.sb(f"gt{i}", [128, T], BF16) for i in range(2)]
        pm = self.A("pm")
        col_list, meta = [], []
        f = 0
        while f < c.FT:
            n = min(2, c.FT - f)
            for j in range(n):
                col_list.append((f + j) * 128)
                meta.append(("a", f + j))
            for j in range(n):
                col_list.append(c.F + (f + j) * 128)
                meta.append(("b", f + j))
            f += n

        def epi(i, banks):
            kind, f = meta[i]
            a = asb[f % 4]
            if kind == "a":
                self.evac(banks, groups, a, dst_off=2, first_write=False)
                return
            acc_, s_, g_ = ac[f % 2], sa[f % 2], gt[f % 2]
            cw = lambda tap: self.A("cw", (l * 3 + tap) * c.FT + f)
            cb = self.A("cb", l * c.FT + f)
            P.op("dve", lambda e: e.tensor_scalar(out=acc_[:, :], in0=a[:, 2:T + 2], scalar1=cw(2), scalar2=cb, op0=ALU.mult, op1=ALU.add),
                 reads=[a, self.tabA], writes=[acc_])
            P.op("dve", lambda e: e.scalar_tensor_tensor(out=acc_[:, :], in0=a[:, 1:T + 1], scalar=cw(1), in1=acc_[:, :], op0=ALU.mult, op1=ALU.add),
                 reads=[a, self.tabA], writes=[acc_])
            P.op("dve", lambda e: e.scalar_tensor_tensor(out=acc_[:, :], in0=a[:, 0:T], scalar=cw(0), in1=acc_[:, :], op0=ALU.mult, op1=ALU.add),
                 reads=[a, self.tabA], writes=[acc_])
            P.op("act", lambda e: e.activation(out=s_[:, :], in_=acc_[:, :], func=AF.Silu), reads=[acc_], writes=[s_])
            for gi, (g0, gn) in enumerate(groups):
                P.op("dve", lambda e, gi=gi, g0=g0, gn=gn: e.tensor_tensor(out=g_[:, g0:g0 + gn], in0=banks[gi][:, 0:gn], in1=s_[:, g0:g0 + gn], op=ALU.mult),
                     reads=[banks[gi], s_], writes=[g_] if gi == 0 else (), acc=[g_] if gi else ())
            P.op("dve", lambda e: e.tensor_scalar(out=g_[:, 0:PRE], in0=g_[:, 0:PRE], scalar1=pm, scalar2=None, op0=ALU.mult),
                 reads=[self.tabA], writes=[g_])
            P.dma("sp", self.gated.t.ap()[f * 128:(f + 1) * 128, :], g_[:, :], g_, reads=[g_], acc=[self.gated])
        self.gemm(("w_up", l), xb, 0, c.KT, col_list, groups, epi, nslots=4)

    def stage_ffn_down(self, l):
        P, c = self.P, self.c
        T = c.T
        groups = groups_of(T)
        ngr = len(c.kgroups)
        for gidx, (k0, nk) in enumerate(c.kgroups):
            P.stage()
            gx = P.sb("gx", [128, nk, T], BF16)
            gv_ = self.gated.t.ap()[k0 * 128:(k0 + nk) * 128, :].rearrange("(kt p) t -> p kt t", p=128)
            for k1 in range(0, nk, 16):
                k2 = min(nk, k1 + 16)
                P.dma("sp", gx[:, k1:k2, :], gv_[:, k1:k2, :], gx, reads=[self.gated], writes=[gx] if k1 == 0 else (), acc=[gx] if k1 else ())
            hres = [P.sb(f"hres{i}", [128, T], F32) for i in range(2)]
            hm = [P.sb(f"hm{i}", [128, T], F32) for i in range(2)]
            src = self.hmid if gidx == 0 else self.hacc
            dst = self.hout[l] if gidx == ngr - 1 else self.hacc

            def epi(nt, banks, src=src, dst=dst):
                hr = hres[nt % 2]
                o = hm[nt % 2]
                P.dma("sp", hr[:, :], src.t.ap()[nt * 128:(nt + 1) * 128, :], hr, reads=[src], writes=[hr])
                for gi, (g0, gn) in enumerate(groups):
                    P.op("dve", lambda e, gi=gi, g0=g0, gn=gn: e.tensor_tensor(out=o[:, g0:g0 + gn], in0=banks[gi][:, 0:gn], in1=hr[:, g0:g0 + gn], op=ALU.add),
                         reads=[banks[gi], hr], writes=[o] if gi == 0 else (), acc=[o] if gi else ())
                P.dma("sp", dst.t.ap()[nt * 128:(nt + 1) * 128, :], o[:, :], o, reads=[o], acc=[dst])
            self.gemm(("w_down", l), gx, k0, nk, [i * 128 for i in range(c.KT)], groups, epi)

    def stage_final(self, hsrc):
        P, c = self.P, self.c
        P.stage()
        T = c.T
        ot = [P.sb(f"ot{i}", [128, T], F32) for i in range(2)]

        def out_fn(kt, t, rstd):
            o = ot[kt % 2]
            g = self.A("gf", kt)
            P.op("dve", lambda e: e.scalar_tensor_tensor(out=o[:, :], in0=t[:, :], scalar=g, in1=rstd[:, :], op0=ALU.mult, op1=ALU.mult),
                 reads=[t, rstd, self.tabA], writes=[o])
            P.dma("sp", self.outT.ap()[kt * 128:(kt + 1) * 128, :], o[:, PRE:T], o, reads=[o], acc=[self.out_res])
        self.rmsnorm(hsrc, None, T, None, out_fn)

    def build(self, upto=None):
        P, c = self.P, self.c
        import os
        stop = os.environ.get("MK_STOP", "")

        class _Stop(Exception):
            pass

        def chk(name):
            if stop == name:
                raise _Stop()
        try:
            self._build_body(chk)
        except _Stop:
            pass
        P.nobar = set()
        P.stage()
        P.build()
        return self.nc

    def _build_body(self, chk):
        P, c = self.P, self.c
        self.stage_init()
        order = []
        for l in range(c.L):
            for k in ("w_in", "pool_w", "w_out", "w_up", "w_down"):
                order.append((k, l))
        self.worder = order
        for key in order[:4]:
            self.wprep(key)
        h = self.xT
        for l in range(c.L):
            self.stage_inproj(l, h)
            chk(f"inproj{l}")
            self.dump_tile(f"qT{l}", self.qT)
            self.dump_tile(f"kT{l}", self.kT)
            self.dump_tile(f"vT{l}", self.vT)
            self.dump_tile(f"pT{l}", self.pT)
            self.stage_retention(l, with_out=False)
            chk(f"retA{l}")
            P.barrier(["pool"])
            P.collective(self.exS[l], self.gS[l])
            P.collective(self.exP[l], self.gP[l])
            self.wprep(("w_down", l))
            if l + 1 < c.L:
                self.wprep(("w_in", l + 1))
                self.wprep(("pool_w", l + 1))
            chk(f"E1{l}")
            self.stage_retention(l, with_out=True)
            chk(f"retB{l}")
            self.stage_pool(l)
            chk(f"pool{l}")
            self.dump_tile(f"mixin{l}", self.mixin)
            self.stage_wout(l, h)
            chk(f"wout{l}")
            P.barrier(["pool"])
            P.collective(self.exH[l], self.gH[l])
            if l + 1 < c.L:
                self.wprep(("w_out", l + 1))
                self.wprep(("w_up", l + 1))
            self.dump_tile(f"hmid{l}", self.hmid)
            self.stage_ffn_up(l)
            chk(f"up{l}")
            self.dump_tile(f"gated{l}", self.gated)
            self.stage_ffn_down(l)
            self.dump_tile(f"hout{l}", self.hout[l])
            h = self.hout[l]
        self.stage_final(h)
        P.stage()
        res = [self.out_res] + list(self.dump_out.values())
        need = {}
        for r in res:
            for s, v in r.w.items():
                need[s] = max(need.get(s, 0), v)
        sems = P.sems

        def emit(e):
            for s, v in need.items():
                e.wait_ge(sems[s], v)
        P.q["sp"].append(emit)


def make_tables(cfg, core):
    c = cfg
    b, j = divmod(core, QPB)
    H = c.H
    tabA = np.zeros((128, c.nA), np.float32)
    tabB = np.zeros((128, c.nB), np.float32)
    lg = np.log1p(-np.exp2(-5.0 - np.arange(H, dtype=np.float64)))
    g128 = np.exp(lg * 128)

    def setA(name, arr):
        o, n = c.tabA[name]
        tabA[:, o:o + n] = np.asarray(arr, np.float32).reshape(-1, n)

    def setB(name, arr):
        o, n = c.tabB[name]
        tabB[:, o:o + n] = np.asarray(arr, np.float32).reshape(-1, n)
    coef = np.zeros((NCORE, H))
    for i in range(NCORE):
        bi, ji = divmod(i, QPB)
        if bi == b and ji < j:
            coef[i] = g128 ** (c.NCH * (j - 1 - ji)) / g128
    setA("coef", np.broadcast_to(coef.reshape(1, -1), (128, NCORE * H)))
    sel = np.zeros(NCORE)
    if j > 0:
        sel[core - 1] = 1.0
    setA("sel", np.broadcast_to(sel.reshape(1, -1), (128, NCORE)))
    invc = np.zeros((4, PRE))
    if j == 0:
        for gi in range(4):
            w = 2 ** (gi + 1)
            invc[gi] = 1.0 / np.minimum(np.arange(PRE) + 1, w)
    setA("invc", np.broadcast_to(invc.reshape(1, -1), (128, 4 * PRE)))
    setA("pm", np.full((128, 1), 1.0 if j == 0 else 0.0))
    setA("npm", np.full((128, 1), -1.0 if j == 0 else 0.0))
    tk = np.arange(128)[:, None]
    tq = np.arange(128)[None, :]
    setA("cmask", (tq >= tk).astype(np.float32))
    setA("ident", np.eye(128, dtype=np.float32))
    setA("ones", np.ones((128, 128), np.float32))
    setA("g128", np.broadcast_to(g128.reshape(1, -1), (128, H)))
    pos = np.zeros(c.T)
    pos[:PRE] = np.arange(PRE)
    pos[PRE:] = PRE + j * c.CH + np.arange(c.CH)
    inv = np.power(10000.0, -np.arange(128, dtype=np.float32) / 128.0).astype(np.float32)
    ang = inv[:, None].astype(np.float32) * pos[None, :].astype(np.float32)
    setB("cos", np.cos(ang.astype(np.float32)))
    setB("sin", np.sin(ang.astype(np.float32)))
    i1 = np.arange(128, dtype=np.float64) + 1.0
    qdec = np.exp(lg[:, None] * i1[None, :])
    kdec = np.exp(-lg[:, None] * i1[None, :]) * (DH ** -0.5)
    setB("qdec", np.broadcast_to(qdec.reshape(1, -1), (128, H * 128)))
    setB("kdec", np.broadcast_to(kdec.reshape(1, -1), (128, H * 128)))
    return tabA, tabB


def fm(v, nt):
    v = np.asarray(v, np.float32)
    lead = v.shape[:-1]
    return np.moveaxis(v.reshape(*lead, nt, 128), -1, 0)


def prepare_inputs(cfg, x, meta_tokens, norm1_g, w_in, pool_w, pool_scale, w_out, norm2_g, w_up, conv_w, conv_b, w_down, final_g):
    c = cfg
    in_maps = []
    g1 = fm(norm1_g, c.KT).reshape(128, -1)
    g2 = fm(norm2_g, c.KT).reshape(128, -1)
    gf = fm(final_g, c.KT).reshape(128, -1)
    psc = fm(pool_scale, c.PWT).reshape(128, -1)
    cw = fm(conv_w, c.FT).reshape(128, -1)
    cb = fm(conv_b, c.FT).reshape(128, -1)
    ws = {"w_in": np.asarray(w_in), "pool_w": np.asarray(pool_w).reshape(c.L, 4 * c.PG, c.PG), "w_out": np.asarray(w_out),
          "w_up": np.asarray(w_up), "w_down": np.asarray(w_down)}
    for core in range(NCORE):
        b, j = divmod(core, QPB)
        tabA, tabB = make_tables(c, core)
        for name, arr in (("g1", g1), ("g2", g2), ("gf", gf), ("psc", psc), ("cw", cw), ("cb", cb)):
            o, n = c.tabA[name]
            tabA[:, o:o + n] = arr
        xT = np.zeros((c.D, c.T), np.float32)
        if j == 0:
            xT[:, :PRE] = np.asarray(meta_tokens, np.float32).T
        xT[:, PRE:] = np.asarray(x[b, j * c.CH:(j + 1) * c.CH, :], np.float32).T
        m = {"xT": xT, "tabA": tabA, "tabB": tabB}
        for l in range(c.L):
            for k, w in ws.items():
                r = w.shape[1] // NCORE
                m[f"{k}{l}_s"] = np.ascontiguousarray(w[l, core * r:(core + 1) * r, :], dtype=np.float32)
        in_maps.append(m)
    return in_maps


def assemble(cfg, results, B):
    c = cfg
    out = np.zeros((B, QPB * c.CH, c.D), np.float32)
    for core in range(NCORE):
        b, j = divmod(core, QPB)
        out[b, j * c.CH:(j + 1) * c.CH, :] = results[core]["outT"].T
    return out


_CACHE = {}


def kernel(x, meta_tokens, norm1_g, w_in, pool_w, pool_scale, w_out, norm2_g, w_up, conv_w, conv_b, w_down, final_g):
    cfg = Cfg()
    if "nc" not in _CACHE:
        _CACHE["nc"] = Builder(cfg).build()
    nc = _CACHE["nc"]
    in_maps = prepare_inputs(cfg, x, meta_tokens, norm1_g, w_in, pool_w, pool_scale, w_out, norm2_g, w_up, conv_w, conv_b, w_down, final_g)
    res = run_bass_kernel_spmd(nc, in_maps, core_ids=list(range(NCORE)))
    return assemble(cfg, res.results, np.asarray(x).shape[0])
```
